# Optimizing a Trainium2 kernel written in Bass

```python
import jax, jax.numpy as jnp
from jax import lax
import numpy as np

D_MODEL = 1024
BATCH = 16
SEQ = 2048
DEPTH = 4

N_A = DEPTH // 2
N_B = DEPTH - N_A

GLA_HEADS = 4
GLA_QK = D_MODEL // 2
GLA_V = D_MODEL
GLA_HEAD_K = GLA_QK // GLA_HEADS
GLA_HEAD_V = GLA_V // GLA_HEADS
GATE_RANK = 16
GATE_TAU = 16.0
GLA_CHUNK = 64
GLA_IN = 2 * GLA_QK + 2 * GLA_V + GATE_RANK

SB_HEADS = 16
SB_HEAD_DIM = D_MODEL // SB_HEADS
BLOCK_Q = 128

D_FF = 4 * D_MODEL

DEEPNORM_ALPHA = (2.0 * DEPTH) ** 0.25
DEEPNORM_BETA = (8.0 * DEPTH) ** -0.25
LN_EPS = 1e-5
RMS_EPS = 1e-6

kernel_name = "yoco_gla_stickbreaking_deepnorm"


def layer_norm(x, g, b):
    xf = x.astype(jnp.float32)
    mu = jnp.mean(xf, axis=-1, keepdims=True)
    var = jnp.mean(jnp.square(xf - mu), axis=-1, keepdims=True)
    y = (xf - mu) * lax.rsqrt(var + LN_EPS)
    return (y * g.astype(jnp.float32) + b.astype(jnp.float32)).astype(x.dtype)


def gla_chunked(q, k, v, log_a):
    B, S, H, dk = q.shape
    dv = v.shape[-1]
    n = S // GLA_CHUNK

    def to_chunks(t):
        return t.astype(jnp.float32).reshape(B, n, GLA_CHUNK, H, t.shape[-1]).transpose(0, 3, 1, 2, 4)

    qc = to_chunks(q) * (dk ** -0.5)
    kc = to_chunks(k)
    vc = to_chunks(v)
    b = jnp.cumsum(to_chunks(log_a), axis=3)
    b_last = b[:, :, :, -1:, :]
    q_dec = qc * jnp.exp(b)
    k_dec = kc * jnp.exp(-b)
    k_end = kc * jnp.exp(b_last - b)

    causal = jnp.tril(jnp.ones((GLA_CHUNK, GLA_CHUNK), dtype=bool))
    att = jnp.where(causal, jnp.einsum('bhncd,bhnsd->bhncs', q_dec, k_dec), 0.0)
    o_intra = jnp.einsum('bhncs,bhnse->bhnce', att, vc)

    u = jnp.einsum('bhnsd,bhnse->bhnde', k_end, vc)
    decay = jnp.exp(b_last[:, :, :, 0, :])

    def step(state, inp):
        d_n, u_n = inp
        return d_n[..., None] * state + u_n, state

    init = jnp.zeros((B, H, dk, dv), jnp.float32)
    _, s_prev = lax.scan(step, init, (jnp.moveaxis(decay, 2, 0), jnp.moveaxis(u, 2, 0)))
    s_prev = jnp.moveaxis(s_prev, 0, 2)
    o_inter = jnp.einsum('bhncd,bhnde->bhnce', q_dec, s_prev)
    o = o_intra + o_inter
    return o.transpose(0, 2, 3, 1, 4).reshape(B, S, H, dv)


def gla_mixer(x, w_in, w_a2, b_a, norm_g, w_o):
    B, S, _ = x.shape
    proj = x @ w_in
    q, k, v, g, a1 = jnp.split(
        proj, [GLA_QK, 2 * GLA_QK, 2 * GLA_QK + GLA_V, 2 * GLA_QK + 2 * GLA_V], axis=-1)
    log_a = jax.nn.log_sigmoid((a1 @ w_a2 + b_a).astype(jnp.float32)) / GATE_TAU
    hk = lambda t: t.reshape(B, S, GLA_HEADS, GLA_HEAD_K)
    o = gla_chunked(hk(q), hk(k), v.reshape(B, S, GLA_HEADS, GLA_HEAD_V), hk(log_a))
    o = o * lax.rsqrt(jnp.mean(jnp.square(o), axis=-1, keepdims=True) + RMS_EPS)
    o = o * norm_g.astype(jnp.float32)
    o = o * jax.nn.silu(g.astype(jnp.float32).reshape(B, S, GLA_HEADS, GLA_HEAD_V))
    return o.reshape(B, S, GLA_V).astype(x.dtype) @ w_o


def stick_breaking_attention(q, k, v):
    S = q.shape[1]
    scale = SB_HEAD_DIM ** -0.5
    outs = []
    for i in range(S // BLOCK_Q):
        kv_len = (i + 1) * BLOCK_Q
        q_blk = q[:, i * BLOCK_Q:kv_len].astype(jnp.float32)
        k_blk = k[:, :kv_len].astype(jnp.float32)
        v_blk = v[:, :kv_len].astype(jnp.float32)
        z = jnp.einsum('bqhd,bkhd->bhqk', q_blk, k_blk) * scale
        t_idx = i * BLOCK_Q + jnp.arange(BLOCK_Q)
        s_idx = jnp.arange(kv_len)
        mask = s_idx[None, :] < t_idx[:, None]
        log_1m = jnp.where(mask, jax.nn.log_sigmoid(-z), 0.0)
        rev = lax.cumsum(log_1m, axis=3, reverse=True)
        log_w = jax.nn.log_sigmoid(z) + (rev - log_1m)
        w = jnp.where(mask, jnp.exp(log_w), 0.0)
        outs.append(jnp.einsum('bhqk,bkhd->bqhd', w, v_blk))
    return jnp.concatenate(outs, axis=1)


def sq_relu_mlp(x, w_up, w_down):
    h = jax.nn.relu(x @ w_up)
    return (h * h) @ w_down


def setup_inputs(seed: int = 0) -> dict:
    key = jax.random.key(seed)
    ks = jax.random.split(key, 16)
    nrm = lambda k, shape, s: jax.random.normal(k, shape, jnp.float32) * s
    return {
        "x": nrm(ks[0], (BATCH, SEQ, D_MODEL), 1.0),
        "gla_w_in": nrm(ks[1], (N_A, D_MODEL, GLA_IN), D_MODEL ** -0.5),
        "gla_w_a2": nrm(ks[2], (N_A, GATE_RANK, GLA_QK), GATE_RANK ** -0.5),
        "gla_b_a": nrm(ks[3], (N_A, GLA_QK), 0.1),
        "gla_norm_g": 1.0 + nrm(ks[4], (N_A, GLA_HEAD_V), 0.02),
        "gla_w_o": nrm(ks[5], (N_A, GLA_V, D_MODEL), GLA_V ** -0.5 * DEEPNORM_BETA),
        "kv_w": nrm(ks[6], (D_MODEL, 2 * D_MODEL), D_MODEL ** -0.5),
        "sb_w_q": nrm(ks[7], (N_B, D_MODEL, D_MODEL), D_MODEL ** -0.5),
        "sb_w_o": nrm(ks[8], (N_B, D_MODEL, D_MODEL), D_MODEL ** -0.5 * DEEPNORM_BETA),
        "mlp_w_up": nrm(ks[9], (DEPTH, D_MODEL, D_FF), D_MODEL ** -0.5),
        "mlp_w_down": nrm(ks[10], (DEPTH, D_FF, D_MODEL), D_FF ** -0.5 * DEEPNORM_BETA),
        "ln_g": 1.0 + nrm(ks[11], (DEPTH, 2, D_MODEL), 0.02),
        "ln_b": nrm(ks[12], (DEPTH, 2, D_MODEL), 0.02),
    }


def reference(x, gla_w_in, gla_w_a2, gla_b_a, gla_norm_g, gla_w_o, kv_w, sb_w_q, sb_w_o,
              mlp_w_up, mlp_w_down, ln_g, ln_b):
    B, S, D = x.shape
    h = x
    k_sh = None
    v_sh = None
    for layer in range(DEPTH):
        if layer < N_A:
            mix = gla_mixer(h, gla_w_in[layer], gla_w_a2[layer], gla_b_a[layer],
                            gla_norm_g[layer], gla_w_o[layer])
        else:
            j = layer - N_A
            if j == 0:
                kv = h @ kv_w
                k_sh = kv[..., :D].reshape(B, S, SB_HEADS, SB_HEAD_DIM)
                v_sh = kv[..., D:].reshape(B, S, SB_HEADS, SB_HEAD_DIM)
            q = (h @ sb_w_q[j]).reshape(B, S, SB_HEADS, SB_HEAD_DIM)
            o = stick_breaking_attention(q, k_sh, v_sh)
            mix = o.reshape(B, S, D).astype(h.dtype) @ sb_w_o[j]
        h = layer_norm(DEEPNORM_ALPHA * h + mix, ln_g[layer, 0], ln_b[layer, 0])
        h = layer_norm(DEEPNORM_ALPHA * h + sq_relu_mlp(h, mlp_w_up[layer], mlp_w_down[layer]),
                       ln_g[layer, 1], ln_b[layer, 1])
    return h
```

```python
import contextlib
import numpy as np
import concourse.bass as bass
import concourse.mybir as mybir
from concourse.bass_utils import run_bass_kernel_spmd

F32 = mybir.dt.float32
BF16 = mybir.dt.bfloat16
AF = mybir.ActivationFunctionType
ALU = mybir.AluOpType

D = 1024
S = 2048
NT = S // 128
DFF = 4096
GLA_IN = 3088
ALPHA = float(8.0 ** 0.25)
LN_EPS = 1e-5
RMS_EPS = 1e-6
NCORES = 8
SEQ_PER_CORE = 2


class V:
    __slots__ = ("ap", "keys")

    def __init__(self, ap, keys):
        self.ap = ap
        self.keys = tuple(keys)


def esz(dt):
    return 4 if dt == F32 else 2


class Arena:
    def __init__(self, nc, nbytes):
        t = nc.alloc_sbuf_tensor("arena", [128, nbytes // 4], F32)
        self.ap = t.ap() if hasattr(t, "ap") else t[:]
        self.nbytes = nbytes
        self.ptr = 0

    def alloc(self, n, dt):
        nb = (n * esz(dt) + 31) // 32 * 32
        off = self.ptr
        self.ptr += nb
        assert self.ptr <= self.nbytes, f"arena overflow {self.ptr}"
        a = self.ap[:, off // 4:(off + nb) // 4]
        if dt != F32:
            a = a.bitcast(dt)
        a = a[:, 0:n]
        return FV(a, "sb", off, dt)


class FV(V):
    __slots__ = ("space", "off", "dt")

    def __init__(self, ap, space, off, dt):
        n = ap.shape[-1]
        if space == "ps":
            b = off // 2048
            V.__init__(self, ap, ((space, b * 2048, (b + 1) * 2048),))
        else:
            V.__init__(self, ap, ((space, off, off + n * esz(dt)),))
        self.space, self.off, self.dt = space, off, dt

    def c(self, a, b, p0=None, p1=None):
        ap = self.ap[:, a:b] if p0 is None else self.ap[p0:p1, a:b]
        return FV(ap, self.space, self.off + a * esz(self.dt), self.dt)

    def p(self, p0, p1):
        return FV(self.ap[p0:p1, :], self.space, self.off, self.dt)

    def r(self, pat, **kw):
        return V(self.ap.rearrange(pat, **kw), self.keys)

    def bc(self, dt):
        assert self.dt == F32 and dt == BF16
        return FV(self.ap.bitcast(dt), self.space, self.off, dt)


class Sched:
    E = ("pe", "act", "dve", "pool", "sp")

    def __init__(self):
        self.ops = {e: [] for e in self.E}
        self.recs = {}
        self.by_space = {}
        self.ov = {}
        self.seen = {e: {} for e in self.E}
        self.dmacnt = {}

    def _ovl(self, key):
        lst = self.ov.get(key)
        if lst is None:
            sp, a, b = key
            lst = []
            allk = self.by_space.setdefault(sp, [])
            for k2 in allk:
                if k2[1] < b and a < k2[2]:
                    lst.append(k2)
                    self.ov[k2].append(key)
            lst.append(key)
            allk.append(key)
            self.ov[key] = lst
            self.recs[key] = [None, {}]
        return lst

    def add(self, eng, fn, reads=(), writes=(), dma=None):
        ops = self.ops[eng]
        idx = len(ops)
        deps = []
        for v in reads:
            for key in v.keys:
                for k in self._ovl(key):
                    w = self.recs[k][0]
                    if w is not None:
                        deps.append((w, True))
        for v in writes:
            for key in v.keys:
                for k in self._ovl(key):
                    r = self.recs[k]
                    if r[0] is not None:
                        deps.append((r[0], True))
                    for d in r[1].values():
                        deps.append((d, False))
        if dma:
            cnt = self.dmacnt.get(dma, 0) + 1
            self.dmacnt[dma] = cnt
            me = ("dma", dma, cnt)
            rk = ("dma", dma)
        else:
            me = ("op", eng, idx)
            rk = eng
        seen = self.seen[eng]
        waits = {}
        if dma and cnt > 1:
            deps.append((("dma", dma, cnt - 1), True))
        for d, strong in deps:
            if d[0] == "op" and d[1] == eng and not dma:
                if eng == "pe" or not strong:
                    continue
            k = (d[0], d[1])
            if seen.get(k, -1) >= d[2]:
                continue
            if waits.get(k, -1) < d[2]:
                waits[k] = d[2]
        for k, val in waits.items():
            seen[k] = val
            if k[0] == "op":
                self.ops[k[1]][val]["sig"] = True
        ops.append(dict(fn=fn, waits=waits, sig=False, dma=dma))
        for v in reads:
            for key in v.keys:
                self.recs[key][1][rk] = me
        for v in writes:
            for key in v.keys:
                self.recs[key][0] = me
                self.recs[key][1] = {}

    def emit(self, nc, final_waits):
        ranks = {}
        for e in self.E:
            r = 0
            rk = []
            for op in self.ops[e]:
                if op["sig"]:
                    r += 1
                rk.append(r)
            ranks[e] = rk
        with contextlib.ExitStack() as st:
            sems = {e: st.enter_context(nc.semaphore("s_" + e)) for e in self.E}
            dsems = {n: st.enter_context(nc.semaphore("d_" + n)) for n in self.dmacnt}
            block = st.enter_context(nc.Block())

            def run(eng_name):
                def body(eng):
                    for op in self.ops[eng_name]:
                        for k, val in op["waits"].items():
                            if k[0] == "op":
                                eng.wait_ge(sems[k[1]], ranks[k[1]][val])
                            else:
                                eng.wait_ge(dsems[k[1]], 16 * val)
                        ins = op["fn"](eng)
                        if op["dma"]:
                            ins.then_inc(dsems[op["dma"]], 16)
                        elif op["sig"]:
                            ins.then_inc(sems[eng_name], 1)
                    if eng_name == "sp":
                        for n in final_waits:
                            eng.wait_ge(dsems[n], 16 * self.dmacnt[n])
                return body

            block.tensor(run("pe"))
            block.scalar(run("act"))
            block.vector(run("dve"))
            block.gpsimd(run("pool"))
            block.sync(run("sp"))


def build_nc(n_sub=8, nseq=SEQ_PER_CORE):
    nc = bass.Bass("TRN2", target_bir_lowering=False)
    dr = lambda name, shape, dt=F32, kind="ExternalInput": nc.dram_tensor(name, shape, dt, kind=kind).ap()
    x_d = dr("x", [nseq, S, D])
    out_d = dr("out", [nseq, S, D], kind="ExternalOutput")
    w_in_d = dr("gla_w_in", [2, D, GLA_IN])
    w_a2_d = dr("gla_w_a2", [2, 16, 512])
    b_a_d = dr("gla_b_a", [2, 512])
    ng_d = dr("gla_norm_g", [2, 256])
    gwo_d = dr("gla_w_o", [2, D, D])
    kvw_d = dr("kv_w", [D, 2 * D])
    wq_d = dr("sb_w_q", [2, D, D])
    swo_d = dr("sb_w_o", [2, D, D])
    wup_d = dr("mlp_w_up", [4, D, DFF])
    wdn_d = dr("mlp_w_down", [4, DFF, D])
    lng_d = dr("ln_g", [4, 2, D])
    lnb_d = dr("ln_b", [4, 2, D])
    cst_d = dr("consts", [128, 1152])
    kT_d = [nc.dram_tensor(f"kT_scr{s}", [8, 128, S], BF16).ap() for s in range(nseq)]
    v_d = [nc.dram_tensor(f"v_scr{s}", [8, 128, NT, 128], BF16).ap() for s in range(nseq)]

    A = Arena(nc, 207 * 1024)
    Sc = Sched()
    add = Sc.add

    ACC = A.alloc(NT * D, F32)
    acc3 = ACC.ap.rearrange("p (t d) -> p t d", t=NT)
    HT = A.alloc(NT * 8 * 128, BF16)
    ht4 = HT.ap.rearrange("p (t c k) -> p t c k", t=NT, c=8)
    CST = A.alloc(1152, F32)
    identf, trif, gmask, sbmask = CST.c(0, 128), CST.c(128, 256), CST.c(256, 768), CST.c(768, 896)
    notmask = CST.c(896, 1024)
    negmf = CST.c(1024, 1152)
    ident = A.alloc(128, BF16)
    tri = A.alloc(128, BF16)
    negm = A.alloc(128, BF16)
    LNG = A.alloc(D, F32)
    LNB = A.alloc(D, F32)
    stats2 = [A.alloc(16, F32) for _ in range(2)]
    small2 = [A.alloc(16, F32) for _ in range(2)]
    PH = A.ptr

    def acc_t(t, a=0, b=D):
        return ACC.c(t * D + a, t * D + b)

    def ht_t(t):
        return HT.c(t * 1024, (t + 1) * 1024)

    def ht_lhs(t, c):
        return V(ht4[:, t, c, :], ht_t(t).keys)

    def ht_rhs(tg, c):
        return V(ht4[:, 4 * tg:4 * tg + 4, c, :], HT.c(tg * 4096, (tg + 1) * 4096).keys)

    psb = []
    for i in range(8):
        t = nc.alloc_psum_tensor(f"ps{i}", [128, 512], F32)
        psb.append(FV(t.ap() if hasattr(t, "ap") else t[:], "ps", i * 2048, F32))

    def mm(out, lhsT, rhs, start=True, stop=True):
        add("pe", lambda e: e.matmul(out.ap, lhsT=lhsT.ap, rhs=rhs.ap, start=start, stop=stop),
            reads=(lhsT, rhs), writes=(out,))

    def tp(out, in_, idv):
        add("pe", lambda e: e.transpose(out=out.ap, in_=in_.ap, identity=idv.ap), reads=(in_, idv), writes=(out,))

    def act(out, in_, func, bias=None, scale=None, accum=None, extra_r=()):
        kw = {}
        rd = [in_] + list(extra_r)
        if bias is not None:
            kw["bias"] = bias.ap if isinstance(bias, V) else bias
            if isinstance(bias, V):
                rd.append(bias)
        if scale is not None:
            kw["scale"] = scale.ap if isinstance(scale, V) else scale
            if isinstance(scale, V):
                rd.append(scale)
        wr = [out]
        if accum is not None:
            kw["accum_out"] = accum.ap
            wr.append(accum)
        add("act", lambda e: e.activation(out=out.ap, in_=in_.ap, func=func, **kw), reads=rd, writes=wr)

    def tt(eng, out, a, b, op):
        add(eng, lambda e: e.tensor_tensor(out=out.ap, in0=a.ap, in1=b.ap, op=op), reads=(a, b), writes=(out,))

    def ts(eng, out, a, s1, s2, op0, op1=None):
        rd = [a] + [s for s in (s1, s2) if isinstance(s, V)]
        g = lambda s: s.ap if isinstance(s, V) else s
        if op1 is None:
            add(eng, lambda e: e.tensor_scalar(out=out.ap, in0=a.ap, scalar1=g(s1), scalar2=None, op0=op0),
                reads=rd, writes=(out,))
        else:
            add(eng, lambda e: e.tensor_scalar(out=out.ap, in0=a.ap, scalar1=g(s1), scalar2=g(s2), op0=op0, op1=op1),
                reads=rd, writes=(out,))

    def stt(out, a, s, b, op0, op1):
        rd = [a, b] + ([s] if isinstance(s, V) else [])
        g = lambda q: q.ap if isinstance(q, V) else q
        add("dve", lambda e: e.scalar_tensor_tensor(out=out.ap, in0=a.ap, scalar=g(s), in1=b.ap, op0=op0, op1=op1),
            reads=rd, writes=(out,))

    def cp(eng, out, in_):
        if eng == "act":
            add("act", lambda e: e.copy(out=out.ap, in_=in_.ap), reads=(in_,), writes=(out,))
        else:
            add(eng, lambda e: e.tensor_copy(out=out.ap, in_=in_.ap), reads=(in_,), writes=(out,))

    def memset(eng, v, val):
        add(eng, lambda e: e.memset(v.ap, val), writes=(v,))

    def dma(q, out, in_, sem, reads=(), writes=()):
        add(q, lambda e: e.dma_start(out=out, in_=in_), reads=reads, writes=writes, dma=sem)

    dma("sp", CST.ap, cst_d, "cst", writes=(CST,))
    cp("dve", ident, identf)
    cp("dve", tri, trif)
    cp("dve", negm, negmf)

    def ln_tail(t, hb, tbank, final_store=None):
        at = acc_t(t)
        stats, small = stats2[t % 2], small2[t % 2]
        if isinstance(hb, (list, tuple)):
            hb = hb[t % 2]
        st6 = stats.c(0, 12)
        mv = stats.c(12, 14)
        add("dve", lambda e: e.bn_stats(out=stats.ap[:, 0:6], in_=acc3[:, t, 0:512]), reads=(acc_t(t, 0, 512),), writes=(stats.c(0, 6),))
        add("dve", lambda e: e.bn_stats(out=stats.ap[:, 6:12], in_=acc3[:, t, 512:1024]), reads=(acc_t(t, 512, 1024),), writes=(stats.c(6, 12),))
        add("dve", lambda e: e.bn_aggr(out=mv.ap, in_=st6.ap), reads=(st6,), writes=(mv,))
        lnv, rstd, nmr = small.c(0, 1), small.c(1, 2), small.c(2, 3)
        act(lnv, stats.c(13, 14), AF.Ln, bias=LN_EPS)
        act(rstd, lnv, AF.Exp, scale=-0.5)
        ts("dve", nmr, stats.c(12, 13), -1.0, rstd, ALU.mult, ALU.mult)
        act(at, at, AF.Identity, bias=nmr, scale=rstd)
        tt("dve", at, at, LNG, ALU.mult)
        tt("dve", at, at, LNB, ALU.add)
        if final_store is not None:
            dma("sp", final_store, acc3[:, t, :], f"out{t % 4}", reads=(at,))
            return
        to_ht(t, hb, tbank)

    def to_ht(t, hb, tbank):
        at = acc_t(t)
        cp("act", hb, at)
        pb = tbank.bc(BF16)
        for c in range(8):
            tp(pb.c(c * 128, (c + 1) * 128), hb.c(c * 128, (c + 1) * 128), ident)
        cp("dve", ht_t(t), pb)

    def load_ln(l, j):
        dma("sp", LNG.ap, lng_d[l, j, :].partition_broadcast(128), "lnp", writes=(LNG,))
        dma("sp", LNB.ap, lnb_d[l, j, :].partition_broadcast(128), "lnp", writes=(LNB,))

    def wload(dst, src, sem):
        dma("pool", dst.ap, src, sem, writes=(dst,))

    def out_proj_tail(t, ysb, yT, W, hb, final_store=None, tpb=7, mixb=(5, 6), lnb=7):
        pb = psb[tpb].bc(BF16)
        for c in range(8):
            tp(pb.c(c * 128, (c + 1) * 128), ysb.c(c * 128, (c + 1) * 128), ident)
        cp("act", yT, pb)
        for half in range(2):
            bank = psb[mixb[half]]
            for c in range(8):
                mm(bank, yT.c(c * 128, (c + 1) * 128),
                   V(W.ap.rearrange("p (c d) -> p c d", c=8)[:, c, half * 512:(half + 1) * 512], W.keys),
                   start=(c == 0), stop=(c == 7))
            a = acc_t(t, half * 512, (half + 1) * 512)
            stt(a, a, ALPHA, bank, ALU.mult, ALU.add)
        ln_tail(t, hb, psb[lnb], final_store)

    def gla_layer(s, l):
        A.ptr = PH
        WQK = A.alloc(8 * 1024, BF16)
        WV = A.alloc(8 * 1024, BF16)
        WGA = A.alloc(8 * 1040, BF16)
        wblk = ((WQK, 0, 1024), (WV, 1024, 1024), (WGA, 2048, 1040))
        WO = A.alloc(8 * D, BF16)
        WA2 = A.alloc(512, BF16)
        NG = A.alloc(256, F32)
        SF = A.alloc(1024, F32)
        SB = A.alloc(1024, BF16)
        a1T = A.alloc(128, BF16)
        espA = A.alloc(1024, F32)
        esp = espA.c(0, 512)
        sphl = espA.c(512, 1024).bc(BF16)
        sp_hi, sp_lo = sphl.c(0, 512), sphl.c(512, 1024)
        Eb, Enb = espA.c(0, 512), espA.c(512, 1024)
        qkd = A.alloc(1024, BF16)
        q_dec, k_dec = qkd.c(0, 512), qkd.c(512, 1024)
        qkT = A.alloc(1024, BF16)
        v_sb = A.alloc(1024, BF16)
        attT = A.alloc(512, BF16)
        junk = attT.c(0, 256)
        decay = A.alloc(4, F32)
        ss = A.alloc(4, F32)
        rinv = A.alloc(4, F32)
        Abuf = A.alloc(1024, F32)
        Ebuf = A.alloc(1024, F32)
        ysb = A.alloc(1024, BF16)
        hb = ysb
        yT = A.alloc(1024, BF16)

        for bi in (2, 0, 1):
            blk, c0, wdt = wblk[bi]
            wload(V(blk.ap.rearrange("p (c f) -> p c f", c=8), blk.keys),
                  w_in_d[l].rearrange("(c p) f -> p c f", p=128)[:, :, c0:c0 + wdt], f"win{bi}")
        wload(V(WO.ap.rearrange("p (c d) -> p c d", c=8), WO.keys), gwo_d[l].rearrange("(c p) d -> p c d", p=128), "wo")
        memset("dve", WA2, 0.0)
        wload(WA2.p(0, 16), w_a2_d[l], "wa2")
        wload(WA2.p(16, 17), b_a_d[l:l + 1, :], "wa2")
        dma("sp", NG.ap, ng_d[l, :].partition_broadcast(128), "ng", writes=(NG,))
        memset("dve", a1T, 1.0)
        memset("dve", SF, 0.0)
        memset("dve", SB, 0.0)
        load_ln(l, 0)

        def wcol(c, a, b):
            for blk, c0, wdt in wblk:
                if c0 <= a and b <= c0 + wdt:
                    return V(blk.ap.rearrange("p (c f) -> p c f", c=8)[:, c, a - c0:b - c0], blk.keys)
            raise AssertionError((a, b))

        P = psb

        def obank(h):
            return P[1 + h // 2].c((h % 2) * 256, (h % 2) * 256 + 256)

        def segA(t):
            a1p = P[7].c(0, 128, 0, 16)
            for c in range(8):
                mm(a1p, wcol(c, 3072, 3088), ht_lhs(t, c), start=(c == 0), stop=(c == 7))
            cp("act", a1T.p(0, 16), a1p)
            mm(P[0], a1T.p(0, 32), WA2.p(0, 32))
            act(esp, P[0], AF.Exp, scale=-1.0)
            act(esp, esp, AF.Ln, bias=1.0)
            cp("act", sp_hi, esp)
            tt("dve", sp_lo, esp, sp_hi, ALU.subtract)

        def segB(t):
            for j in range(2):
                for c in range(8):
                    mm(P[3 + j], ht_lhs(t, c), wcol(c, 1024 + j * 512, 1024 + (j + 1) * 512), start=(c == 0), stop=(c == 7))
                cp("act", v_sb.c(j * 512, (j + 1) * 512), P[3 + j])
            for j in range(2):
                for c in range(8):
                    mm(P[1 + j], ht_lhs(t, c), wcol(c, j * 512, (j + 1) * 512), start=(c == 0), stop=(c == 7))

        def segC(t):
            mm(P[0], tri, sp_hi, start=True, stop=False)
            mm(P[0], tri, sp_lo, start=False, stop=True)
            blp = P[7].c(128, 132)
            for h in range(4):
                mm(blp.c(h, h + 1), sp_hi.c(h * 128, (h + 1) * 128), tri.c(127, 128), start=True, stop=False)
                mm(blp.c(h, h + 1), sp_lo.c(h * 128, (h + 1) * 128), tri.c(127, 128), start=False, stop=True)
            act(Eb, P[0], AF.Exp)
            act(Enb, P[0], AF.Exp, scale=-1.0)
            act(decay, blp, AF.Exp)
            stt(q_dec, P[1], float(128.0 ** -0.5), Eb, ALU.mult, ALU.mult)
            tt("dve", k_dec, P[2], Enb, ALU.mult)

        def segD(t):
            pb7 = P[7].bc(BF16)
            for h in range(4):
                tp(pb7.c(h * 128, (h + 1) * 128), q_dec.c(h * 128, (h + 1) * 128), ident)
            for h in range(4):
                tp(pb7.c(512 + h * 128, 512 + (h + 1) * 128), k_dec.c(h * 128, (h + 1) * 128), ident)
            cp("dve", qkT, pb7)

        def segE(t):
            for h in range(4):
                mm(P[0].c(h * 128, (h + 1) * 128), qkT.c(512 + h * 128, 512 + (h + 1) * 128),
                   qkT.c(h * 128, (h + 1) * 128))
            tt("dve", attT, P[0], gmask, ALU.mult)

        def segF(t):
            for h in range(4):
                ub = P[3 + h // 2].c((h % 2) * 256, (h % 2) * 256 + 256)
                mm(ub, k_dec.c(h * 128, (h + 1) * 128), v_sb.c(h * 256, (h + 1) * 256))
            for h in range(4):
                mm(obank(h), attT.c(h * 128, (h + 1) * 128), v_sb.c(h * 256, (h + 1) * 256), start=True, stop=False)
                mm(obank(h), qkT.c(h * 128, (h + 1) * 128), SB.c(h * 256, (h + 1) * 256), start=False, stop=True)
            for j in range(2):
                tt("dve", SF.c(j * 512, (j + 1) * 512), P[3 + j], SF.c(j * 512, (j + 1) * 512), ALU.add)
            for h in range(4):
                act(SF.c(h * 256, (h + 1) * 256), SF.c(h * 256, (h + 1) * 256), AF.Copy, scale=decay.c(h, h + 1))
            cp("act", SB, SF)

        def partA2(t):
            for h in range(4):
                act(junk, obank(h), AF.Square, accum=ss.c(h, h + 1))
            act(rinv, ss, AF.Ln, bias=RMS_EPS, scale=1.0 / 256.0)
            act(rinv, rinv, AF.Exp, scale=-0.5)
            for h in range(4):
                stt(Abuf.c(h * 256, (h + 1) * 256), obank(h), rinv.c(h, h + 1), NG, ALU.mult, ALU.mult)

        def segP(t):
            for j in range(2):
                for c in range(8):
                    mm(P[5 + j], ht_lhs(t, c), wcol(c, 2048 + j * 512, 2048 + (j + 1) * 512), start=(c == 0), stop=(c == 7))
            for j in range(2):
                act(Ebuf.c(j * 512, (j + 1) * 512), P[5 + j], AF.Exp, scale=-1.0)
            act(Ebuf, Ebuf, AF.Ln, bias=1.0)
            act(Ebuf, Ebuf, AF.Exp, scale=-1.0)
            for j in range(2):
                tt("dve", Ebuf.c(j * 512, (j + 1) * 512), P[5 + j], Ebuf.c(j * 512, (j + 1) * 512), ALU.mult)
            tt("dve", ysb, Abuf, Ebuf, ALU.mult)

        def segQ(t):
            pb = psb[5].bc(BF16)
            for c in range(8):
                tp(pb.c(c * 128, (c + 1) * 128), ysb.c(c * 128, (c + 1) * 128), ident)
            cp("act", yT, pb)
            for half in range(2):
                bank = psb[(6, 5)[half]]
                for c in range(8):
                    mm(bank, yT.c(c * 128, (c + 1) * 128),
                       V(WO.ap.rearrange("p (c d) -> p c d", c=8)[:, c, half * 512:(half + 1) * 512], WO.keys),
                       start=(c == 0), stop=(c == 7))
                a = acc_t(t, half * 512, (half + 1) * 512)
                stt(a, a, ALPHA, bank, ALU.mult, ALU.add)

        def segR(t):
            ln_tail(t, hb, psb[6])

        for sg in (segA, segB, segC, segD, segE, segF):
            sg(0)
        partA2(0)
        for t in range(NT):
            n = t + 1 < NT
            if n:
                segA(t + 1)
            segP(t)
            if n:
                segB(t + 1)
                segC(t + 1)
            segQ(t)
            if n:
                segD(t + 1)
                segE(t + 1)
            segR(t)
            if n:
                segF(t + 1)
                partA2(t + 1)

    def mlp_layer(s, l, final):
        A.ptr = PH
        aT = [A.alloc(4 * S, BF16) for _ in range(2)]
        WUP = [A.alloc(8 * 512, BF16) for _ in range(2)]
        WDN = [A.alloc(4 * D, BF16) for _ in range(2)]
        rb = [A.alloc(512, F32) for _ in range(2)]
        hb = [A.alloc(D, BF16) for _ in range(2)]
        load_ln(l, 1)
        NFG = DFF // 512

        def load_w(fg):
            sl = fg % 2
            wload(V(WUP[sl].ap.rearrange("p (c f) -> p c f", c=8), WUP[sl].keys),
                  wup_d[l].rearrange("(c p) f -> p c f", p=128)[:, :, fg * 512:(fg + 1) * 512], f"wup{sl}")
            wload(V(WDN[sl].ap.rearrange("p (c d) -> p c d", c=4), WDN[sl].keys),
                  wdn_d[l, fg * 512:(fg + 1) * 512, :].rearrange("(c p) d -> p c d", p=128), f"wdn{sl}")

        cnt = [0, 0]

        def up(fg):
            sl = fg % 2
            w3 = WUP[sl].ap.rearrange("p (c f) -> p c f", c=8)
            for fi in range(4):
                for tg in range(4):
                    bank = psb[cnt[0] % 3]
                    k = cnt[0] % 2
                    cnt[0] += 1
                    o3 = V(bank.ap.rearrange("p (a b) -> p a b", a=4), bank.keys)
                    for c in range(8):
                        mm(o3, V(w3[:, c, fi * 128:(fi + 1) * 128], WUP[sl].keys), ht_rhs(tg, c),
                           start=(c == 0), stop=(c == 7))
                    act(rb[k], bank, AF.Relu)
                    stt(aT[sl].c(fi * S + tg * 512, fi * S + (tg + 1) * 512), bank, 0.0, rb[k], ALU.max, ALU.mult)

        def down(fg):
            sl = fg % 2
            w3 = WDN[sl].ap.rearrange("p (c d) -> p c d", c=4)
            for t in range(NT):
                for half in range(2):
                    bank = psb[3 + cnt[1] % 4]
                    cnt[1] += 1
                    for fc in range(4):
                        mm(bank, aT[sl].c(fc * S + t * 128, fc * S + (t + 1) * 128),
                           V(w3[:, fc, half * 512:(half + 1) * 512], WDN[sl].keys), start=(fc == 0), stop=(fc == 3))
                    a = acc_t(t, half * 512, (half + 1) * 512)
                    if fg == 0:
                        stt(a, a, ALPHA, bank, ALU.mult, ALU.add)
                    else:
                        tt("dve", a, a, bank, ALU.add)
                if fg == NFG - 1:
                    ln_tail(t, hb, psb[(7, 0, 1, 2)[t % 4]], final_store=(out_d[s, t * 128:(t + 1) * 128, :] if final else None))

        load_w(0)
        load_w(1)
        up(0)
        for fg in range(NFG):
            if fg + 1 < NFG:
                up(fg + 1)
            down(fg)
            if fg + 2 < NFG:
                load_w(fg + 2)

    def kv_proj(s):
        A.ptr = PH
        WK = A.alloc(8 * D, BF16)
        WV = A.alloc(8 * D, BF16)
        kst = [A.alloc(S, BF16) for _ in range(2)]
        vst = [A.alloc(D, BF16) for _ in range(2)]
        wk3 = WK.ap.rearrange("p (c d) -> p c d", c=8)
        wv3 = WV.ap.rearrange("p (c d) -> p c d", c=8)
        kvr = kvw_d.rearrange("(c p) f -> p c f", p=128)
        wload(V(wk3, WK.keys), kvr[:, :, 0:D], "wk")
        wload(V(wv3, WV.keys), kvr[:, :, D:2 * D], "wv")
        n = 0
        for oc in range(8):
            st = kst[oc % 2]
            for tg in range(4):
                bank = psb[n % 4]
                n += 1
                o3 = V(bank.ap.rearrange("p (a b) -> p a b", a=4), bank.keys)
                for c in range(8):
                    mm(o3, V(wk3[:, c, oc * 128:(oc + 1) * 128], WK.keys), ht_rhs(tg, c), start=(c == 0), stop=(c == 7))
                cp("act" if tg % 2 else "dve", st.c(tg * 512, (tg + 1) * 512), bank)
            dma("sp", kT_d[s][oc], st.ap, f"kst{oc % 2}", reads=(st,), writes=(V(None, ((("dr", "k", s, oc), 0, 1),)),))
        for t in range(NT):
            st = vst[t % 2]
            for half in range(2):
                bank = psb[4 + n % 4]
                n += 1
                for c in range(8):
                    mm(bank, ht_lhs(t, c), V(wv3[:, c, half * 512:(half + 1) * 512], WV.keys), start=(c == 0), stop=(c == 7))
                cp("act" if half else "dve", st.c(half * 512, (half + 1) * 512), bank)
            dma("sp", v_d[s][:, :, t, :].rearrange("h p c -> p h c"), st.ap.rearrange("p (h c) -> p h c", h=8),
                f"vst{t % 2}", reads=(st,), writes=(V(None, ((("dr", "v", s), 0, 1),)),))

    def sb_layer(s, j):
        A.ptr = PH
        QK = A.alloc(8 * D, BF16)
        QT = [QK.c(k * S, (k + 1) * S) for k in range(2)]
        KT = [QK.c((2 + k) * S, (3 + k) * S) for k in range(2)]
        WQO = QK
        wq3 = WQO.ap.rearrange("p (c d) -> p c d", c=8)
        WQ = [A.alloc(8 * 128, BF16) for _ in range(2)]
        VH = [A.alloc(NT * 128, BF16) for _ in range(2)]
        OSB = A.alloc(NT * D, BF16)
        ones = A.alloc(512, F32)
        eb = [A.alloc(512, F32) for _ in range(3)]
        Pb = [A.alloc(516, F32) for _ in range(3)]
        tmpb = [A.alloc(512, F32) for _ in range(3)]
        wb = [A.alloc(512, BF16) for _ in range(3)]
        wTb = [A.alloc(512, BF16) for _ in range(3)]
        NR = A.alloc(8, F32)
        zero1 = A.alloc(2, F32)
        yT = [eb[0].bc(BF16), tmpb[0].bc(BF16)]
        hb = [eb[1].bc(BF16), eb[2].bc(BF16)]
        memset("dve", ones, 1.0)
        memset("dve", zero1, 0.0)
        for k in range(3):
            memset("dve", Pb[k].c(0, 1), 0.0)

        def load_wq(hp):
            wload(V(WQ[hp % 2].ap.rearrange("p (c d) -> p c d", c=8), WQ[hp % 2].keys),
                  wq_d[j].rearrange("(c p) d -> p c d", p=128)[:, :, hp * 128:(hp + 1) * 128], f"wq{hp % 2}")

        load_wq(0)
        load_wq(1)
        load_ln(2 + j, 0)
        zbanks = (psb[0], psb[1], psb[2], psb[3])
        obanks = (psb[4], psb[6])
        tbanks = (psb[5], psb[7])
        gq = [0]
        oi = [0]
        def prologue(hp):
            sl = hp % 2
            dma("sp", KT[sl].ap, kT_d[s][hp], f"kt{sl}", reads=(V(None, ((("dr", "k", s, hp), 0, 1),)),), writes=(KT[sl],))
            dma("sp", VH[sl].ap.rearrange("p (t c) -> p t c", t=NT), v_d[s][hp], f"vh{sl}",
                reads=(V(None, ((("dr", "v", s), 0, 1),)),), writes=(VH[sl],))
            for tg in range(4):
                bank = psb[0]
                o3 = V(bank.ap.rearrange("p (a b) -> p a b", a=4), bank.keys)
                for c in range(8):
                    mm(o3, WQ[sl].c(c * 128, (c + 1) * 128), ht_rhs(tg, c), start=(c == 0), stop=(c == 7))
                act(QT[sl].c(tg * 512, (tg + 1) * 512), bank, AF.Copy, scale=0.125)
            if hp + 2 < 8:
                load_wq(hp + 2)

        prologue(0)
        for hp in range(8):
            sl = hp % 2
            if hp + 1 < 8:
                prologue(hp + 1)
            vh3 = VH[sl].ap.rearrange("p (t c) -> p t c", t=NT)
            groups = []
            for hh in range(2):
                for i in range(NT):
                    gl = i // 4
                    glist = [(4 * gl, i - 4 * gl + 1, True)] + [(4 * g, 4, False) for g in range(gl - 1, -1, -1)]
                    for gidx, (kb0, nb, diag) in enumerate(glist):
                        groups.append(dict(hh=hh, i=i, kb0=kb0, nb=nb, diag=diag, first=(gidx == 0),
                                           last=(gidx == len(glist) - 1), oslot=oi[0], q=gq[0]))
                        gq[0] += 1
                    oi[0] += 1

            def s1(g):
                q, N = g["q"], 128 * g["nb"]
                r0 = 64 * g["hh"]
                z = zbanks[q % 4].c(0, N)
                mm(z, QT[sl].c(g["i"] * 128, (g["i"] + 1) * 128, r0, r0 + 64),
                   KT[sl].c(g["kb0"] * 128, g["kb0"] * 128 + N, r0, r0 + 64), start=True, stop=not g["diag"])
                if g["diag"]:
                    mm(z.c(N - 128, N), ident, negm, start=False, stop=True)
                act(eb[q % 3].c(0, N), z, AF.Sigmoid, scale=-1.0)
                act(z, z, AF.Sigmoid)

            def s2(g):
                q, N = g["q"], 128 * g["nb"]
                om = eb[q % 3].c(0, N)
                Pk = Pb[q % 3]
                prev = ones.c(0, 1) if g["first"] else Pb[(q - 1) % 3].c(0, 1)
                cp("pool", Pk.c(N, N + 1), prev)
                add("dve", lambda en, Pk=Pk, om=om, N=N, prev=prev: en.tensor_tensor_scan(
                    out=Pk.ap[:, 0:N][:, ::-1], data0=om.ap[:, ::-1], data1=ones.ap[:, 0:N], initial=prev.ap,
                    op0=ALU.mult, op1=ALU.mult), reads=(ones, om, prev), writes=(Pk.c(0, N),))

            def s3(g):
                q, N = g["q"], 128 * g["nb"]
                w = wb[q % 3].c(0, N)
                tt("dve", w, zbanks[q % 4].c(0, N), Pb[q % 3].c(1, N + 1), ALU.mult)

            def s4(g):
                q, N = g["q"], 128 * g["nb"]
                w = wb[q % 3].c(0, N)
                pT = tbanks[q % 2].bc(BF16)
                for b in range(g["nb"]):
                    tp(pT.c(b * 128, (b + 1) * 128), w.c(b * 128, (b + 1) * 128), ident)
                cp("act", wTb[q % 3].c(0, N), pT.c(0, N))

            def s5(g):
                q, N = g["q"], 128 * g["nb"]
                r0 = 64 * g["hh"]
                oreg = obanks[g["oslot"] % 2].c((g["oslot"] // 2 % 8) * 64, (g["oslot"] // 2 % 8) * 64 + 64)
                wT = wTb[q % 3].c(0, N)
                for b in range(g["nb"]):
                    mm(oreg, wT.c(b * 128, (b + 1) * 128), V(vh3[:, g["kb0"] + b, r0:r0 + 64], VH[sl].keys),
                       start=(g["first"] and b == 0), stop=(g["last"] and b == g["nb"] - 1))
                if g["last"]:
                    col = g["i"] * D + hp * 128 + r0
                    cp("act", OSB.c(col, col + 64), oreg)

            stages = (s1, s2, s3, s4, s5)
            for step in range(len(groups) + len(stages) - 1):
                for k, st in enumerate(stages):
                    idx = step - k
                    if 0 <= idx < len(groups):
                        st(groups[idx])
        wload(V(wq3, WQO.keys), swo_d[j].rearrange("(c p) d -> p c d", p=128), "wqo")
        for t in range(NT):
            if t % 2 == 0:
                out_proj_tail(t, OSB.c(t * D, (t + 1) * D), yT[0], WQO, hb, tpb=7, mixb=(5, 6), lnb=7)
            else:
                out_proj_tail(t, OSB.c(t * D, (t + 1) * D), yT[1], WQO, hb, tpb=0, mixb=(1, 2), lnb=0)

    for s in range(nseq):
        A.ptr = PH
        hb0 = [A.alloc(D, BF16) for _ in range(2)]
        for t in range(NT):
            dma("sp", acc3[:, t, :], x_d[s, t * 128:(t + 1) * 128, :], f"x{t % 4}", writes=(acc_t(t),))
            to_ht(t, hb0[t % 2], psb[6 + t % 2])
        sub = 0
        for l in range(4):
            if sub >= n_sub:
                break
            if l < 2:
                gla_layer(s, l)
            else:
                if l == 2:
                    kv_proj(s)
                sb_layer(s, l - 2)
            sub += 1
            if sub >= n_sub:
                break
            mlp_layer(s, l, final=(l == 3))
            sub += 1
        if n_sub < 8:
            for t in range(NT):
                dma("sp", out_d[s, t * 128:(t + 1) * 128, :], acc3[:, t, :], f"out{t % 4}", reads=(acc_t(t),))
    Sc.emit(nc, final_waits=("out0", "out1", "out2", "out3"))
    build_nc.last_sched = Sc
    return nc


def make_consts():
    c = np.zeros((128, 1152), np.float32)
    c[:, 0:128] = np.eye(128, dtype=np.float32)
    i = np.arange(128)
    tri = (i[:, None] <= i[None, :]).astype(np.float32)
    c[:, 128:256] = tri * (-1.0 / 16.0)
    c[:, 256:768] = np.tile(tri, (1, 4))
    c[:, 768:896] = (i[None, :] < i[:, None]).astype(np.float32)
    c[:, 896:1024] = 1.0 - c[:, 768:896]
    c[:, 1024:1152] = -30000.0 * c[:, 896:1024]
    return c


_NC_CACHE = {}


def kernel(x, gla_w_in, gla_w_a2, gla_b_a, gla_norm_g, gla_w_o, kv_w, sb_w_q, sb_w_o,
           mlp_w_up, mlp_w_down, ln_g, ln_b, _n_sub=8, _ncores=NCORES):
    f = lambda a: np.ascontiguousarray(np.asarray(a, dtype=np.float32))
    x = f(x)
    shared = dict(gla_w_in=f(gla_w_in), gla_w_a2=f(gla_w_a2), gla_b_a=f(gla_b_a), gla_norm_g=f(gla_norm_g),
                  gla_w_o=f(gla_w_o), kv_w=f(kv_w), sb_w_q=f(sb_w_q), sb_w_o=f(sb_w_o), mlp_w_up=f(mlp_w_up),
                  mlp_w_down=f(mlp_w_down), ln_g=f(ln_g), ln_b=f(ln_b), consts=make_consts())
    key = (_n_sub,)
    if key not in _NC_CACHE:
        _NC_CACHE[key] = build_nc(_n_sub)
    nc = _NC_CACHE[key]
    in_maps = []
    for i in range(_ncores):
        m = dict(shared)
        m["x"] = np.ascontiguousarray(x[i * SEQ_PER_CORE:(i + 1) * SEQ_PER_CORE])
        in_maps.append(m)
    res = run_bass_kernel_spmd(nc, in_maps, core_ids=list(range(_ncores)))
    return np.concatenate([r["out"] for r in res.results], axis=0)
```

```python
import contextlib
import numpy as np
import concourse.bass as bass
import concourse.mybir as mybir
from concourse.bass_utils import run_bass_kernel_spmd

F32 = mybir.dt.float32
BF16 = mybir.dt.bfloat16
AF = mybir.ActivationFunctionType
ALU = mybir.AluOpType

D = 1024
S = 2048
NT = S // 128
DFF = 4096
GLA_IN = 3088
ALPHA = float(8.0 ** 0.25)
LN_EPS = 1e-5
RMS_EPS = 1e-6
NCORES = 8
SEQ_PER_CORE = 2


class V:
    __slots__ = ("ap", "keys")

    def __init__(self, ap, keys):
        self.ap = ap
        self.keys = tuple(keys)


def esz(dt):
    return 4 if dt == F32 else 2


class Arena:
    def __init__(self, nc, nbytes):
        t = nc.alloc_sbuf_tensor("arena", [128, nbytes // 4], F32)
        self.ap = t.ap() if hasattr(t, "ap") else t[:]
        self.nbytes = nbytes
        self.ptr = 0

    def alloc(self, n, dt):
        nb = (n * esz(dt) + 31) // 32 * 32
        off = self.ptr
        self.ptr += nb
        assert self.ptr <= self.nbytes, f"arena overflow {self.ptr}"
        a = self.ap[:, off // 4:(off + nb) // 4]
        if dt != F32:
            a = a.bitcast(dt)
        a = a[:, 0:n]
        return FV(a, "sb", off, dt)


class FV(V):
    __slots__ = ("space", "off", "dt")

    def __init__(self, ap, space, off, dt):
        n = ap.shape[-1]
        if space == "ps":
            b = off // 2048
            V.__init__(self, ap, ((space, b * 2048, (b + 1) * 2048),))
        else:
            V.__init__(self, ap, ((space, off, off + n * esz(dt)),))
        self.space, self.off, self.dt = space, off, dt

    def c(self, a, b, p0=None, p1=None):
        ap = self.ap[:, a:b] if p0 is None else self.ap[p0:p1, a:b]
        return FV(ap, self.space, self.off + a * esz(self.dt), self.dt)

    def p(self, p0, p1):
        return FV(self.ap[p0:p1, :], self.space, self.off, self.dt)

    def r(self, pat, **kw):
        return V(self.ap.rearrange(pat, **kw), self.keys)

    def bc(self, dt):
        assert self.dt == F32 and dt == BF16
        return FV(self.ap.bitcast(dt), self.space, self.off, dt)


class Sched:
    E = ("pe", "act", "dve", "pool", "sp")

    def __init__(self):
        self.ops = {e: [] for e in self.E}
        self.recs = {}
        self.by_space = {}
        self.ov = {}
        self.seen = {e: {} for e in self.E}
        self.dmacnt = {}

    def _ovl(self, key):
        lst = self.ov.get(key)
        if lst is None:
            sp, a, b = key
            lst = []
            allk = self.by_space.setdefault(sp, [])
            for k2 in allk:
                if k2[1] < b and a < k2[2]:
                    lst.append(k2)
                    self.ov[k2].append(key)
            lst.append(key)
            allk.append(key)
            self.ov[key] = lst
            self.recs[key] = [None, {}]
        return lst

    def add(self, eng, fn, reads=(), writes=(), dma=None):
        ops = self.ops[eng]
        idx = len(ops)
        deps = []
        for v in reads:
            for key in v.keys:
                for k in self._ovl(key):
                    w = self.recs[k][0]
                    if w is not None:
                        deps.append((w, True))
        for v in writes:
            for key in v.keys:
                for k in self._ovl(key):
                    r = self.recs[k]
                    if r[0] is not None:
                        deps.append((r[0], True))
                    for d in r[1].values():
                        deps.append((d, False))
        if dma:
            cnt = self.dmacnt.get(dma, 0) + 1
            self.dmacnt[dma] = cnt
            me = ("dma", dma, cnt)
            rk = ("dma", dma)
        else:
            me = ("op", eng, idx)
            rk = eng
        seen = self.seen[eng]
        waits = {}
        if dma and cnt > 1:
            deps.append((("dma", dma, cnt - 1), True))
        for d, strong in deps:
            if d[0] == "op" and d[1] == eng and not dma:
                if eng == "pe" or not strong:
                    continue
            k = (d[0], d[1])
            if seen.get(k, -1) >= d[2]:
                continue
            if waits.get(k, -1) < d[2]:
                waits[k] = d[2]
        for k, val in waits.items():
            seen[k] = val
            if k[0] == "op":
                self.ops[k[1]][val]["sig"] = True
        ops.append(dict(fn=fn, waits=waits, sig=False, dma=dma))
        for v in reads:
            for key in v.keys:
                self.recs[key][1][rk] = me
        for v in writes:
            for key in v.keys:
                self.recs[key][0] = me
                self.recs[key][1] = {}

    def emit(self, nc, final_waits):
        ranks = {}
        for e in self.E:
            r = 0
            rk = []
            for op in self.ops[e]:
                if op["sig"]:
                    r += 1
                rk.append(r)
            ranks[e] = rk
        with contextlib.ExitStack() as st:
            sems = {e: st.enter_context(nc.semaphore("s_" + e)) for e in self.E}
            dsems = {n: st.enter_context(nc.semaphore("d_" + n)) for n in self.dmacnt}
            block = st.enter_context(nc.Block())

            def run(eng_name):
                def body(eng):
                    for op in self.ops[eng_name]:
                        for k, val in op["waits"].items():
                            if k[0] == "op":
                                eng.wait_ge(sems[k[1]], ranks[k[1]][val])
                            else:
                                eng.wait_ge(dsems[k[1]], 16 * val)
                        ins = op["fn"](eng)
                        if op["dma"]:
                            ins.then_inc(dsems[op["dma"]], 16)
                        elif op["sig"]:
                            ins.then_inc(sems[eng_name], 1)
                    if eng_name == "sp":
                        for n in final_waits:
                            eng.wait_ge(dsems[n], 16 * self.dmacnt[n])
                return body

            block.tensor(run("pe"))
            block.scalar(run("act"))
            block.vector(run("dve"))
            block.gpsimd(run("pool"))
            block.sync(run("sp"))


def build_nc(n_sub=8, nseq=SEQ_PER_CORE):
    nc = bass.Bass("TRN2", target_bir_lowering=False)
    dr = lambda name, shape, dt=F32, kind="ExternalInput": nc.dram_tensor(name, shape, dt, kind=kind).ap()
    x_d = dr("x", [nseq, S, D])
    out_d = dr("out", [nseq, S, D], kind="ExternalOutput")
    w_in_d = dr("gla_w_in", [2, D, GLA_IN])
    w_a2_d = dr("gla_w_a2", [2, 16, 512])
    b_a_d = dr("gla_b_a", [2, 512])
    ng_d = dr("gla_norm_g", [2, 256])
    gwo_d = dr("gla_w_o", [2, D, D])
    kvw_d = dr("kv_w", [D, 2 * D])
    wq_d = dr("sb_w_q", [2, D, D])
    swo_d = dr("sb_w_o", [2, D, D])
    wup_d = dr("mlp_w_up", [4, D, DFF])
    wdn_d = dr("mlp_w_down", [4, DFF, D])
    lng_d = dr("ln_g", [4, 2, D])
    lnb_d = dr("ln_b", [4, 2, D])
    cst_d = dr("consts", [128, 1152])
    kT_d = [nc.dram_tensor(f"kT_scr{s}", [8, 128, S], BF16).ap() for s in range(nseq)]
    v_d = [nc.dram_tensor(f"v_scr{s}", [8, 128, NT, 128], BF16).ap() for s in range(nseq)]

    A = Arena(nc, 207 * 1024)
    Sc = Sched()
    add = Sc.add

    ACC = A.alloc(NT * D, F32)
    acc3 = ACC.ap.rearrange("p (t d) -> p t d", t=NT)
    HT = A.alloc(NT * 8 * 128, BF16)
    ht4 = HT.ap.rearrange("p (t c k) -> p t c k", t=NT, c=8)
    CST = A.alloc(1152, F32)
    identf, trif, gmask, sbmask = CST.c(0, 128), CST.c(128, 256), CST.c(256, 768), CST.c(768, 896)
    notmask = CST.c(896, 1024)
    negmf = CST.c(1024, 1152)
    ident = A.alloc(128, BF16)
    tri = A.alloc(128, BF16)
    negm = A.alloc(128, BF16)
    LNG = A.alloc(D, F32)
    LNB = A.alloc(D, F32)
    stats2 = [A.alloc(16, F32) for _ in range(2)]
    small2 = [A.alloc(16, F32) for _ in range(2)]
    PH = A.ptr

    def acc_t(t, a=0, b=D):
        return ACC.c(t * D + a, t * D + b)

    def ht_t(t):
        return HT.c(t * 1024, (t + 1) * 1024)

    def ht_lhs(t, c):
        return V(ht4[:, t, c, :], ht_t(t).keys)

    def ht_rhs(tg, c):
        return V(ht4[:, 4 * tg:4 * tg + 4, c, :], HT.c(tg * 4096, (tg + 1) * 4096).keys)

    psb = []
    for i in range(8):
        t = nc.alloc_psum_tensor(f"ps{i}", [128, 512], F32)
        psb.append(FV(t.ap() if hasattr(t, "ap") else t[:], "ps", i * 2048, F32))

    def mm(out, lhsT, rhs, start=True, stop=True):
        add("pe", lambda e: e.matmul(out.ap, lhsT=lhsT.ap, rhs=rhs.ap, start=start, stop=stop),
            reads=(lhsT, rhs), writes=(out,))

    def tp(out, in_, idv):
        add("pe", lambda e: e.transpose(out=out.ap, in_=in_.ap, identity=idv.ap), reads=(in_, idv), writes=(out,))

    def act(out, in_, func, bias=None, scale=None, accum=None, extra_r=()):
        kw = {}
        rd = [in_] + list(extra_r)
        if bias is not None:
            kw["bias"] = bias.ap if isinstance(bias, V) else bias
            if isinstance(bias, V):
                rd.append(bias)
        if scale is not None:
            kw["scale"] = scale.ap if isinstance(scale, V) else scale
            if isinstance(scale, V):
                rd.append(scale)
        wr = [out]
        if accum is not None:
            kw["accum_out"] = accum.ap
            wr.append(accum)
        add("act", lambda e: e.activation(out=out.ap, in_=in_.ap, func=func, **kw), reads=rd, writes=wr)

    def tt(eng, out, a, b, op):
        add(eng, lambda e: e.tensor_tensor(out=out.ap, in0=a.ap, in1=b.ap, op=op), reads=(a, b), writes=(out,))

    def ts(eng, out, a, s1, s2, op0, op1=None):
        rd = [a] + [s for s in (s1, s2) if isinstance(s, V)]
        g = lambda s: s.ap if isinstance(s, V) else s
        if op1 is None:
            add(eng, lambda e: e.tensor_scalar(out=out.ap, in0=a.ap, scalar1=g(s1), scalar2=None, op0=op0),
                reads=rd, writes=(out,))
        else:
            add(eng, lambda e: e.tensor_scalar(out=out.ap, in0=a.ap, scalar1=g(s1), scalar2=g(s2), op0=op0, op1=op1),
                reads=rd, writes=(out,))

    def stt(out, a, s, b, op0, op1):
        rd = [a, b] + ([s] if isinstance(s, V) else [])
        g = lambda q: q.ap if isinstance(q, V) else q
        add("dve", lambda e: e.scalar_tensor_tensor(out=out.ap, in0=a.ap, scalar=g(s), in1=b.ap, op0=op0, op1=op1),
            reads=rd, writes=(out,))

    def cp(eng, out, in_):
        if eng == "act":
            add("act", lambda e: e.copy(out=out.ap, in_=in_.ap), reads=(in_,), writes=(out,))
        else:
            add(eng, lambda e: e.tensor_copy(out=out.ap, in_=in_.ap), reads=(in_,), writes=(out,))

    def memset(eng, v, val):
        add(eng, lambda e: e.memset(v.ap, val), writes=(v,))

    def dma(q, out, in_, sem, reads=(), writes=()):
        add(q, lambda e: e.dma_start(out=out, in_=in_), reads=reads, writes=writes, dma=sem)

    dma("sp", CST.ap, cst_d, "cst", writes=(CST,))
    cp("dve", ident, identf)
    cp("dve", tri, trif)
    cp("dve", negm, negmf)

    def ln_tail(t, hb, tbank, final_store=None):
        at = acc_t(t)
        stats, small = stats2[t % 2], small2[t % 2]
        if isinstance(hb, (list, tuple)):
            hb = hb[t % 2]
        st6 = stats.c(0, 12)
        mv = stats.c(12, 14)
        add("dve", lambda e: e.bn_stats(out=stats.ap[:, 0:6], in_=acc3[:, t, 0:512]), reads=(acc_t(t, 0, 512),), writes=(stats.c(0, 6),))
        add("dve", lambda e: e.bn_stats(out=stats.ap[:, 6:12], in_=acc3[:, t, 512:1024]), reads=(acc_t(t, 512, 1024),), writes=(stats.c(6, 12),))
        add("dve", lambda e: e.bn_aggr(out=mv.ap, in_=st6.ap), reads=(st6,), writes=(mv,))
        lnv, rstd, nmr = small.c(0, 1), small.c(1, 2), small.c(2, 3)
        act(lnv, stats.c(13, 14), AF.Ln, bias=LN_EPS)
        act(rstd, lnv, AF.Exp, scale=-0.5)
        ts("dve", nmr, stats.c(12, 13), -1.0, rstd, ALU.mult, ALU.mult)
        act(at, at, AF.Identity, bias=nmr, scale=rstd)
        tt("dve", at, at, LNG, ALU.mult)
        tt("dve", at, at, LNB, ALU.add)
        if final_store is not None:
            dma("sp", final_store, acc3[:, t, :], f"out{t % 4}", reads=(at,))
            return
        to_ht(t, hb, tbank)

    def to_ht(t, hb, tbank):
        at = acc_t(t)
        cp("act", hb, at)
        pb = tbank.bc(BF16)
        for c in range(8):
            tp(pb.c(c * 128, (c + 1) * 128), hb.c(c * 128, (c + 1) * 128), ident)
        cp("dve", ht_t(t), pb)

    def load_ln(l, j):
        dma("sp", LNG.ap, lng_d[l, j, :].partition_broadcast(128), "lnp", writes=(LNG,))
        dma("sp", LNB.ap, lnb_d[l, j, :].partition_broadcast(128), "lnp", writes=(LNB,))

    def wload(dst, src, sem):
        dma("pool", dst.ap, src, sem, writes=(dst,))

    def out_proj_tail(t, ysb, yT, W, hb, final_store=None, tpb=7, mixb=(5, 6), lnb=7):
        pb = psb[tpb].bc(BF16)
        for c in range(8):
            tp(pb.c(c * 128, (c + 1) * 128), ysb.c(c * 128, (c + 1) * 128), ident)
        cp("act", yT, pb)
        for half in range(2):
            bank = psb[mixb[half]]
            for c in range(8):
                mm(bank, yT.c(c * 128, (c + 1) * 128),
                   V(W.ap.rearrange("p (c d) -> p c d", c=8)[:, c, half * 512:(half + 1) * 512], W.keys),
                   start=(c == 0), stop=(c == 7))
            a = acc_t(t, half * 512, (half + 1) * 512)
            stt(a, a, ALPHA, bank, ALU.mult, ALU.add)
        ln_tail(t, hb, psb[lnb], final_store)

    def gla_layer(s, l):
        A.ptr = PH
        WQK = A.alloc(8 * 1024, BF16)
        WV = A.alloc(8 * 1024, BF16)
        WGA = A.alloc(8 * 1040, BF16)
        wblk = ((WQK, 0, 1024), (WV, 1024, 1024), (WGA, 2048, 1040))
        WO = A.alloc(8 * D, BF16)
        WA2 = A.alloc(512, BF16)
        NG = A.alloc(256, F32)
        SF = A.alloc(1024, F32)
        SB = A.alloc(1024, BF16)
        a1T = A.alloc(128, BF16)
        espA = A.alloc(1024, F32)
        esp = espA.c(0, 512)
        sphl = espA.c(512, 1024).bc(BF16)
        sp_hi, sp_lo = sphl.c(0, 512), sphl.c(512, 1024)
        Eb, Enb = espA.c(0, 512), espA.c(512, 1024)
        qkd = A.alloc(1024, BF16)
        q_dec, k_dec = qkd.c(0, 512), qkd.c(512, 1024)
        qkT = A.alloc(1024, BF16)
        v_sb = A.alloc(1024, BF16)
        attT = A.alloc(512, BF16)
        junk = attT.c(0, 256)
        decay = A.alloc(4, F32)
        ss = A.alloc(4, F32)
        rinv = A.alloc(4, F32)
        Abuf = A.alloc(1024, F32)
        Ebuf = A.alloc(1024, F32)
        ysb = A.alloc(1024, BF16)
        hb = ysb
        yT = A.alloc(1024, BF16)

        for bi in (2, 0, 1):
            blk, c0, wdt = wblk[bi]
            wload(V(blk.ap.rearrange("p (c f) -> p c f", c=8), blk.keys),
                  w_in_d[l].rearrange("(c p) f -> p c f", p=128)[:, :, c0:c0 + wdt], f"win{bi}")
        wload(V(WO.ap.rearrange("p (c d) -> p c d", c=8), WO.keys), gwo_d[l].rearrange("(c p) d -> p c d", p=128), "wo")
        memset("dve", WA2, 0.0)
        wload(WA2.p(0, 16), w_a2_d[l], "wa2")
        wload(WA2.p(16, 17), b_a_d[l:l + 1, :], "wa2")
        dma("sp", NG.ap, ng_d[l, :].partition_broadcast(128), "ng", writes=(NG,))
        memset("dve", a1T, 1.0)
        memset("dve", SF, 0.0)
        memset("dve", SB, 0.0)
        load_ln(l, 0)

        def wcol(c, a, b):
            for blk, c0, wdt in wblk:
                if c0 <= a and b <= c0 + wdt:
                    return V(blk.ap.rearrange("p (c f) -> p c f", c=8)[:, c, a - c0:b - c0], blk.keys)
            raise AssertionError((a, b))

        P = psb

        def obank(h):
            return P[1 + h // 2].c((h % 2) * 256, (h % 2) * 256 + 256)

        def segA(t):
            a1p = P[7].c(0, 128, 0, 16)
            for c in range(8):
                mm(a1p, wcol(c, 3072, 3088), ht_lhs(t, c), start=(c == 0), stop=(c == 7))
            cp("act", a1T.p(0, 16), a1p)
            mm(P[0], a1T.p(0, 32), WA2.p(0, 32))
            act(esp, P[0], AF.Exp, scale=-1.0)
            act(esp, esp, AF.Ln, bias=1.0)
            cp("act", sp_hi, esp)
            tt("dve", sp_lo, esp, sp_hi, ALU.subtract)

        def segB(t):
            for j in range(2):
                for c in range(8):
                    mm(P[3 + j], ht_lhs(t, c), wcol(c, 1024 + j * 512, 1024 + (j + 1) * 512), start=(c == 0), stop=(c == 7))
                cp("act", v_sb.c(j * 512, (j + 1) * 512), P[3 + j])
            for j in range(2):
                for c in range(8):
                    mm(P[1 + j], ht_lhs(t, c), wcol(c, j * 512, (j + 1) * 512), start=(c == 0), stop=(c == 7))

        def segC(t):
            mm(P[0], tri, sp_hi, start=True, stop=False)
            mm(P[0], tri, sp_lo, start=False, stop=True)
            blp = P[7].c(128, 132)
            for h in range(4):
                mm(blp.c(h, h + 1), sp_hi.c(h * 128, (h + 1) * 128), tri.c(127, 128), start=True, stop=False)
                mm(blp.c(h, h + 1), sp_lo.c(h * 128, (h + 1) * 128), tri.c(127, 128), start=False, stop=True)
            act(Eb, P[0], AF.Exp)
            act(Enb, P[0], AF.Exp, scale=-1.0)
            act(decay, blp, AF.Exp)
            stt(q_dec, P[1], float(128.0 ** -0.5), Eb, ALU.mult, ALU.mult)
            tt("dve", k_dec, P[2], Enb, ALU.mult)

        def segD(t):
            pb7 = P[7].bc(BF16)
            for h in range(4):
                tp(pb7.c(h * 128, (h + 1) * 128), q_dec.c(h * 128, (h + 1) * 128), ident)
            for h in range(4):
                tp(pb7.c(512 + h * 128, 512 + (h + 1) * 128), k_dec.c(h * 128, (h + 1) * 128), ident)
            cp("dve", qkT, pb7)

        def segE(t):
            for h in range(4):
                mm(P[0].c(h * 128, (h + 1) * 128), qkT.c(512 + h * 128, 512 + (h + 1) * 128),
                   qkT.c(h * 128, (h + 1) * 128))
            tt("dve", attT, P[0], gmask, ALU.mult)

        def segF(t):
            for h in range(4):
                ub = P[3 + h // 2].c((h % 2) * 256, (h % 2) * 256 + 256)
                mm(ub, k_dec.c(h * 128, (h + 1) * 128), v_sb.c(h * 256, (h + 1) * 256))
            for h in range(4):
                mm(obank(h), attT.c(h * 128, (h + 1) * 128), v_sb.c(h * 256, (h + 1) * 256), start=True, stop=False)
                mm(obank(h), qkT.c(h * 128, (h + 1) * 128), SB.c(h * 256, (h + 1) * 256), start=False, stop=True)
            for j in range(2):
                tt("dve", SF.c(j * 512, (j + 1) * 512), P[3 + j], SF.c(j * 512, (j + 1) * 512), ALU.add)
            for h in range(4):
                act(SF.c(h * 256, (h + 1) * 256), SF.c(h * 256, (h + 1) * 256), AF.Copy, scale=decay.c(h, h + 1))
            cp("act", SB, SF)

        def partA2(t):
            for h in range(4):
                act(junk, obank(h), AF.Square, accum=ss.c(h, h + 1))
            act(rinv, ss, AF.Ln, bias=RMS_EPS, scale=1.0 / 256.0)
            act(rinv, rinv, AF.Exp, scale=-0.5)
            for h in range(4):
                stt(Abuf.c(h * 256, (h + 1) * 256), obank(h), rinv.c(h, h + 1), NG, ALU.mult, ALU.mult)

        def segP(t):
            for j in range(2):
                for c in range(8):
                    mm(P[5 + j], ht_lhs(t, c), wcol(c, 2048 + j * 512, 2048 + (j + 1) * 512), start=(c == 0), stop=(c == 7))
            for j in range(2):
                act(Ebuf.c(j * 512, (j + 1) * 512), P[5 + j], AF.Exp, scale=-1.0)
            act(Ebuf, Ebuf, AF.Ln, bias=1.0)
            act(Ebuf, Ebuf, AF.Exp, scale=-1.0)
            for j in range(2):
                tt("dve", Ebuf.c(j * 512, (j + 1) * 512), P[5 + j], Ebuf.c(j * 512, (j + 1) * 512), ALU.mult)
            tt("dve", ysb, Abuf, Ebuf, ALU.mult)

        def segQ(t):
            pb = psb[5].bc(BF16)
            for c in range(8):
                tp(pb.c(c * 128, (c + 1) * 128), ysb.c(c * 128, (c + 1) * 128), ident)
            cp("act", yT, pb)
            for half in range(2):
                bank = psb[(6, 5)[half]]
                for c in range(8):
                    mm(bank, yT.c(c * 128, (c + 1) * 128),
                       V(WO.ap.rearrange("p (c d) -> p c d", c=8)[:, c, half * 512:(half + 1) * 512], WO.keys),
                       start=(c == 0), stop=(c == 7))
                a = acc_t(t, half * 512, (half + 1) * 512)
                stt(a, a, ALPHA, bank, ALU.mult, ALU.add)

        def segR(t):
            ln_tail(t, hb, psb[6])

        for sg in (segA, segB, segC, segD, segE, segF):
            sg(0)
        partA2(0)
        for t in range(NT):
            n = t + 1 < NT
            if n:
                segA(t + 1)
            segP(t)
            if n:
                segB(t + 1)
                segC(t + 1)
            segQ(t)
            if n:
                segD(t + 1)
                segE(t + 1)
            segR(t)
            if n:
                segF(t + 1)
                partA2(t + 1)

    def mlp_layer(s, l, final):
        A.ptr = PH
        aT = [A.alloc(4 * S, BF16) for _ in range(2)]
        WUP = [A.alloc(8 * 512, BF16) for _ in range(2)]
        WDN = [A.alloc(4 * D, BF16) for _ in range(2)]
        rb = [A.alloc(512, F32) for _ in range(2)]
        hb = [A.alloc(D, BF16) for _ in range(2)]
        load_ln(l, 1)
        NFG = DFF // 512

        def load_w(fg):
            sl = fg % 2
            wload(V(WUP[sl].ap.rearrange("p (c f) -> p c f", c=8), WUP[sl].keys),
                  wup_d[l].rearrange("(c p) f -> p c f", p=128)[:, :, fg * 512:(fg + 1) * 512], f"wup{sl}")
            wload(V(WDN[sl].ap.rearrange("p (c d) -> p c d", c=4), WDN[sl].keys),
                  wdn_d[l, fg * 512:(fg + 1) * 512, :].rearrange("(c p) d -> p c d", p=128), f"wdn{sl}")

        cnt = [0, 0]

        def up(fg):
            sl = fg % 2
            w3 = WUP[sl].ap.rearrange("p (c f) -> p c f", c=8)
            for fi in range(4):
                for tg in range(4):
                    bank = psb[cnt[0] % 3]
                    k = cnt[0] % 2
                    cnt[0] += 1
                    o3 = V(bank.ap.rearrange("p (a b) -> p a b", a=4), bank.keys)
                    for c in range(8):
                        mm(o3, V(w3[:, c, fi * 128:(fi + 1) * 128], WUP[sl].keys), ht_rhs(tg, c),
                           start=(c == 0), stop=(c == 7))
                    act(rb[k], bank, AF.Relu)
                    stt(aT[sl].c(fi * S + tg * 512, fi * S + (tg + 1) * 512), bank, 0.0, rb[k], ALU.max, ALU.mult)

        def down(fg):
            sl = fg % 2
            w3 = WDN[sl].ap.rearrange("p (c d) -> p c d", c=4)
            for t in range(NT):
                for half in range(2):
                    bank = psb[3 + cnt[1] % 4]
                    cnt[1] += 1
                    for fc in range(4):
                        mm(bank, aT[sl].c(fc * S + t * 128, fc * S + (t + 1) * 128),
                           V(w3[:, fc, half * 512:(half + 1) * 512], WDN[sl].keys), start=(fc == 0), stop=(fc == 3))
                    a = acc_t(t, half * 512, (half + 1) * 512)
                    if fg == 0:
                        stt(a, a, ALPHA, bank, ALU.mult, ALU.add)
                    else:
                        tt("dve", a, a, bank, ALU.add)
                if fg == NFG - 1:
                    ln_tail(t, hb, psb[(7, 0, 1, 2)[t % 4]], final_store=(out_d[s, t * 128:(t + 1) * 128, :] if final else None))

        load_w(0)
        load_w(1)
        up(0)
        for fg in range(NFG):
            if fg + 1 < NFG:
                up(fg + 1)
            down(fg)
            if fg + 2 < NFG:
                load_w(fg + 2)

    def kv_proj(s):
        A.ptr = PH
        WK = A.alloc(8 * D, BF16)
        WV = A.alloc(8 * D, BF16)
        kst = [A.alloc(S, BF16) for _ in range(2)]
        vst = [A.alloc(D, BF16) for _ in range(2)]
        wk3 = WK.ap.rearrange("p (c d) -> p c d", c=8)
        wv3 = WV.ap.rearrange("p (c d) -> p c d", c=8)
        kvr = kvw_d.rearrange("(c p) f -> p c f", p=128)
        wload(V(wk3, WK.keys), kvr[:, :, 0:D], "wk")
        wload(V(wv3, WV.keys), kvr[:, :, D:2 * D], "wv")
        n = 0
        for oc in range(8):
            st = kst[oc % 2]
            for tg in range(4):
                bank = psb[n % 4]
                n += 1
                o3 = V(bank.ap.rearrange("p (a b) -> p a b", a=4), bank.keys)
                for c in range(8):
                    mm(o3, V(wk3[:, c, oc * 128:(oc + 1) * 128], WK.keys), ht_rhs(tg, c), start=(c == 0), stop=(c == 7))
                cp("act" if tg % 2 else "dve", st.c(tg * 512, (tg + 1) * 512), bank)
            dma("sp", kT_d[s][oc], st.ap, f"kst{oc % 2}", reads=(st,), writes=(V(None, ((("dr", "k", s, oc), 0, 1),)),))
        for t in range(NT):
            st = vst[t % 2]
            for half in range(2):
                bank = psb[4 + n % 4]
                n += 1
                for c in range(8):
                    mm(bank, ht_lhs(t, c), V(wv3[:, c, half * 512:(half + 1) * 512], WV.keys), start=(c == 0), stop=(c == 7))
                cp("act" if half else "dve", st.c(half * 512, (half + 1) * 512), bank)
            dma("sp", v_d[s][:, :, t, :].rearrange("h p c -> p h c"), st.ap.rearrange("p (h c) -> p h c", h=8),
                f"vst{t % 2}", reads=(st,), writes=(V(None, ((("dr", "v", s), 0, 1),)),))

    def sb_layer(s, j):
        A.ptr = PH
        QK = A.alloc(8 * D, BF16)
        QT = [QK.c(k * S, (k + 1) * S) for k in range(2)]
        KT = [QK.c((2 + k) * S, (3 + k) * S) for k in range(2)]
        WQO = QK
        wq3 = WQO.ap.rearrange("p (c d) -> p c d", c=8)
        WQ = [A.alloc(8 * 128, BF16) for _ in range(2)]
        VH = [A.alloc(NT * 128, BF16) for _ in range(2)]
        OSB = A.alloc(NT * D, BF16)
        ones = A.alloc(512, BF16)
        eb = [A.alloc(512, F32) for _ in range(3)]
        Pb = [A.alloc(516, F32) for _ in range(3)]
        tmpb = [A.alloc(512, F32) for _ in range(3)]
        wb = [A.alloc(512, BF16) for _ in range(3)]
        wTb = [A.alloc(512, BF16) for _ in range(3)]
        NR = A.alloc(8, F32)
        zero1 = A.alloc(2, F32)
        yT = [eb[0].bc(BF16), tmpb[0].bc(BF16)]
        hb = [eb[1].bc(BF16), eb[2].bc(BF16)]
        memset("dve", ones, 1.0)
        memset("dve", zero1, 0.0)
        for k in range(3):
            memset("dve", Pb[k].c(0, 1), 0.0)

        def load_wq(hp):
            wload(V(WQ[hp % 2].ap.rearrange("p (c d) -> p c d", c=8), WQ[hp % 2].keys),
                  wq_d[j].rearrange("(c p) d -> p c d", p=128)[:, :, hp * 128:(hp + 1) * 128], f"wq{hp % 2}")

        load_wq(0)
        load_wq(1)
        load_ln(2 + j, 0)
        zbanks = (psb[0], psb[1], psb[2], psb[3])
        obanks = (psb[4], psb[6])
        tbanks = (psb[5], psb[7])
        gq = [0]
        oi = [0]
        for hp in range(8):
            sl = hp % 2
            dma("sp", KT[sl].ap, kT_d[s][hp], f"kt{sl}", reads=(V(None, ((("dr", "k", s, hp), 0, 1),)),), writes=(KT[sl],))
            dma("sp", VH[sl].ap.rearrange("p (t c) -> p t c", t=NT), v_d[s][hp], f"vh{sl}",
                reads=(V(None, ((("dr", "v", s), 0, 1),)),), writes=(VH[sl],))
            for tg in range(4):
                bank = psb[0]
                o3 = V(bank.ap.rearrange("p (a b) -> p a b", a=4), bank.keys)
                for c in range(8):
                    mm(o3, WQ[sl].c(c * 128, (c + 1) * 128), ht_rhs(tg, c), start=(c == 0), stop=(c == 7))
                act(QT[sl].c(tg * 512, (tg + 1) * 512), bank, AF.Copy, scale=0.125)
            if hp + 2 < 8:
                load_wq(hp + 2)
            vh3 = VH[sl].ap.rearrange("p (t c) -> p t c", t=NT)
            groups = []
            for hh in range(2):
                for i in range(NT):
                    gl = i // 4
                    glist = [(4 * gl, i - 4 * gl + 1, True)] + [(4 * g, 4, False) for g in range(gl - 1, -1, -1)]
                    for gidx, (kb0, nb, diag) in enumerate(glist):
                        groups.append(dict(hh=hh, i=i, kb0=kb0, nb=nb, diag=diag, first=(gidx == 0),
                                           last=(gidx == len(glist) - 1), oslot=oi[0], q=gq[0]))
                        gq[0] += 1
                    oi[0] += 1

            def s1(g):
                q, N = g["q"], 128 * g["nb"]
                r0 = 64 * g["hh"]
                z = zbanks[q % 4].c(0, N)
                mm(z, QT[sl].c(g["i"] * 128, (g["i"] + 1) * 128, r0, r0 + 64),
                   KT[sl].c(g["kb0"] * 128, g["kb0"] * 128 + N, r0, r0 + 64), start=True, stop=not g["diag"])
                if g["diag"]:
                    mm(z.c(N - 128, N), ident, negm, start=False, stop=True)
                act(eb[q % 3].c(0, N), z, AF.Sigmoid, scale=-1.0)
                act(z, z, AF.Sigmoid)

            def s2(g):
                q, N = g["q"], 128 * g["nb"]
                om = eb[q % 3].c(0, N)
                Pk = Pb[q % 3]
                prev = ones.c(0, 1) if g["first"] else Pb[(q - 1) % 3].c(0, 1)
                cp("pool", Pk.c(N, N + 1), prev)
                add("dve", lambda en, Pk=Pk, om=om, N=N, prev=prev: en.tensor_tensor_scan(
                    out=Pk.ap[:, 0:N][:, ::-1], data0=om.ap[:, ::-1], data1=ones.ap[:, 0:N], initial=prev.ap,
                    op0=ALU.mult, op1=ALU.mult), reads=(ones, om, prev), writes=(Pk.c(0, N),))

            def s3(g):
                q, N = g["q"], 128 * g["nb"]
                w = wb[q % 3].c(0, N)
                tt("dve", w, zbanks[q % 4].c(0, N), Pb[q % 3].c(1, N + 1), ALU.mult)

            def s4(g):
                q, N = g["q"], 128 * g["nb"]
                w = wb[q % 3].c(0, N)
                pT = tbanks[q % 2].bc(BF16)
                for b in range(g["nb"]):
                    tp(pT.c(b * 128, (b + 1) * 128), w.c(b * 128, (b + 1) * 128), ident)
                cp("act", wTb[q % 3].c(0, N), pT.c(0, N))

            def s5(g):
                q, N = g["q"], 128 * g["nb"]
                r0 = 64 * g["hh"]
                oreg = obanks[g["oslot"] % 2].c((g["oslot"] // 2 % 8) * 64, (g["oslot"] // 2 % 8) * 64 + 64)
                wT = wTb[q % 3].c(0, N)
                for b in range(g["nb"]):
                    mm(oreg, wT.c(b * 128, (b + 1) * 128), V(vh3[:, g["kb0"] + b, r0:r0 + 64], VH[sl].keys),
                       start=(g["first"] and b == 0), stop=(g["last"] and b == g["nb"] - 1))
                if g["last"]:
                    col = g["i"] * D + hp * 128 + r0
                    cp("act", OSB.c(col, col + 64), oreg)

            stages = (s1, s2, s3, s4, s5)
            for step in range(len(groups) + len(stages) - 1):
                for k, st in enumerate(stages):
                    idx = step - k
                    if 0 <= idx < len(groups):
                        st(groups[idx])
        wload(V(wq3, WQO.keys), swo_d[j].rearrange("(c p) d -> p c d", p=128), "wqo")
        for t in range(NT):
            if t % 2 == 0:
                out_proj_tail(t, OSB.c(t * D, (t + 1) * D), yT[0], WQO, hb, tpb=7, mixb=(5, 6), lnb=7)
            else:
                out_proj_tail(t, OSB.c(t * D, (t + 1) * D), yT[1], WQO, hb, tpb=0, mixb=(1, 2), lnb=0)

    for s in range(nseq):
        A.ptr = PH
        hb0 = [A.alloc(D, BF16) for _ in range(2)]
        for t in range(NT):
            dma("sp", acc3[:, t, :], x_d[s, t * 128:(t + 1) * 128, :], f"x{t % 4}", writes=(acc_t(t),))
            to_ht(t, hb0[t % 2], psb[6 + t % 2])
        sub = 0
        for l in range(4):
            if sub >= n_sub:
                break
            if l < 2:
                gla_layer(s, l)
            else:
                if l == 2:
                    kv_proj(s)
                sb_layer(s, l - 2)
            sub += 1
            if sub >= n_sub:
                break
            mlp_layer(s, l, final=(l == 3))
            sub += 1
        if n_sub < 8:
            for t in range(NT):
                dma("sp", out_d[s, t * 128:(t + 1) * 128, :], acc3[:, t, :], f"out{t % 4}", reads=(acc_t(t),))
    Sc.emit(nc, final_waits=("out0", "out1", "out2", "out3"))
    build_nc.last_sched = Sc
    return nc


def make_consts():
    c = np.zeros((128, 1152), np.float32)
    c[:, 0:128] = np.eye(128, dtype=np.float32)
    i = np.arange(128)
    tri = (i[:, None] <= i[None, :]).astype(np.float32)
    c[:, 128:256] = tri * (-1.0 / 16.0)
    c[:, 256:768] = np.tile(tri, (1, 4))
    c[:, 768:896] = (i[None, :] < i[:, None]).astype(np.float32)
    c[:, 896:1024] = 1.0 - c[:, 768:896]
    c[:, 1024:1152] = -30000.0 * c[:, 896:1024]
    return c


_NC_CACHE = {}


def kernel(x, gla_w_in, gla_w_a2, gla_b_a, gla_norm_g, gla_w_o, kv_w, sb_w_q, sb_w_o,
           mlp_w_up, mlp_w_down, ln_g, ln_b, _n_sub=8, _ncores=NCORES):
    f = lambda a: np.ascontiguousarray(np.asarray(a, dtype=np.float32))
    x = f(x)
    shared = dict(gla_w_in=f(gla_w_in), gla_w_a2=f(gla_w_a2), gla_b_a=f(gla_b_a), gla_norm_g=f(gla_norm_g),
                  gla_w_o=f(gla_w_o), kv_w=f(kv_w), sb_w_q=f(sb_w_q), sb_w_o=f(sb_w_o), mlp_w_up=f(mlp_w_up),
                  mlp_w_down=f(mlp_w_down), ln_g=f(ln_g), ln_b=f(ln_b), consts=make_consts())
    key = (_n_sub,)
    if key not in _NC_CACHE:
        _NC_CACHE[key] = build_nc(_n_sub)
    nc = _NC_CACHE[key]
    in_maps = []
    for i in range(_ncores):
        m = dict(shared)
        m["x"] = np.ascontiguousarray(x[i * SEQ_PER_CORE:(i + 1) * SEQ_PER_CORE])
        in_maps.append(m)
    res = run_bass_kernel_spmd(nc, in_maps, core_ids=list(range(_ncores)))
    return np.concatenate([r["out"] for r in res.results], axis=0)
```
